# Optimizing a Trainium2 kernel written in Bass

```python
import math
import jax, jax.numpy as jnp
from jax import lax
import numpy as np

D_MODEL = 2048
BATCH = 1
SEQ = 8192
DEPTH = 2
DEC_BATCH = 16
DEC_SEQ = 64
PAST_LEN = 2048

CHUNK = 64
D_MIX = D_MODEL
SB_WIDTH = D_MIX // 2
SB_HEAD_DIM = 128
SB_HEADS = SB_WIDTH // SB_HEAD_DIM
CONV_WIDTH = D_MIX // 4
CONV_KERNEL = 31
CONV_GROUPS = 4
POOL_WIDTH = D_MIX - SB_WIDTH - CONV_WIDTH
POOL_WINDOWS = (2, 4, 8, 16)
POOL_GROUP = POOL_WIDTH // len(POOL_WINDOWS)
POOL_HIST = max(POOL_WINDOWS) - 1
D_FF = 4 * D_MODEL
Q_BLOCK = 128
IN_COLS = 3 * SB_WIDTH + 2 * CONV_WIDTH + POOL_WIDTH
EPS = 1e-6

kernel_name = "hymba_stickbreak_conformer_pool_stream"


def rmsnorm(x, g):
    xf = x.astype(jnp.float32)
    y = xf * lax.rsqrt(jnp.mean(xf * xf, axis=-1, keepdims=True) + EPS)
    return (y * g.astype(jnp.float32)).astype(x.dtype)


def stick_breaking_block(q, q_pos, k, v, k_pos):
    z = jnp.einsum('bhqd,bhkd->bhqk', q.astype(jnp.float32), k.astype(jnp.float32)) * (SB_HEAD_DIM ** -0.5)
    mask = k_pos[None, :] < q_pos[:, None]
    log_keep = jnp.where(mask, jax.nn.log_sigmoid(-z), 0.0)
    after = lax.cumsum(log_keep, axis=3, reverse=True) - log_keep
    w = jnp.where(mask, jnp.exp(jax.nn.log_sigmoid(z) + after), 0.0)
    return jnp.einsum('bhqk,bhkd->bhqd', w, v.astype(jnp.float32)).astype(v.dtype)


def sb_prompt(q, k, v):
    B, H, T, d = q.shape
    nb = T // Q_BLOCK
    qb = jnp.moveaxis(q.reshape(B, H, nb, Q_BLOCK, d), 2, 0)
    pos = jnp.arange(T, dtype=jnp.int32)
    pb = pos.reshape(nb, Q_BLOCK)
    ob = lax.map(lambda a: stick_breaking_block(a[0], a[1], k, v, pos), (qb, pb))
    return jnp.moveaxis(ob, 0, 2).reshape(B, H, T, d)


def sb_sample(q, k, v, cache_k_l, cache_v_l):
    T = q.shape[2]
    P = cache_k_l.shape[2]
    k_all = jnp.concatenate([cache_k_l.astype(k.dtype), k], axis=2)
    v_all = jnp.concatenate([cache_v_l.astype(v.dtype), v], axis=2)
    k_pos = jnp.arange(P + T, dtype=jnp.int32)
    q_pos = P + jnp.arange(T, dtype=jnp.int32)
    return stick_breaking_block(q, q_pos, k_all, v_all, k_pos)


def conv_module(u, hist, dw_w, dw_b, n_g, n_b, pw_w, pw_b):
    a, b = jnp.split(u, 2, axis=-1)
    glu = a * jax.nn.sigmoid(b)
    ext = jnp.concatenate([hist.astype(glu.dtype), glu], axis=1)
    y = lax.conv_general_dilated(ext, dw_w[:, None, :].astype(ext.dtype), window_strides=(1,),
                                 padding='VALID', dimension_numbers=('NWC', 'WIO', 'NWC'),
                                 feature_group_count=CONV_WIDTH) + dw_b
    B, T, C = y.shape
    yg = y.astype(jnp.float32).reshape(B, T, CONV_GROUPS, C // CONV_GROUPS)
    mu = jnp.mean(yg, axis=-1, keepdims=True)
    var = jnp.mean(jnp.square(yg - mu), axis=-1, keepdims=True)
    yn = ((yg - mu) * lax.rsqrt(var + EPS)).reshape(B, T, C) * n_g.astype(jnp.float32) + n_b.astype(jnp.float32)
    yn = jax.nn.silu(yn).astype(u.dtype)
    return yn @ pw_w + pw_b, ext[:, -(CONV_KERNEL - 1):]


def pool_mixer(u, hist, start_pos, pool_w, pool_scale):
    B, T, C = u.shape
    ext_raw = jnp.concatenate([hist.astype(u.dtype), u], axis=1)
    ext = ext_raw.astype(jnp.float32)
    cs = jnp.concatenate([jnp.zeros((B, 1, C), jnp.float32), jnp.cumsum(ext, axis=1)], axis=1)
    end = cs[:, POOL_HIST + 1:]
    pos = start_pos + jnp.arange(T, dtype=jnp.int32)
    means = []
    for g, w in enumerate(POOL_WINDOWS):
        sl = slice(g * POOL_GROUP, (g + 1) * POOL_GROUP)
        s = end[..., sl] - cs[:, POOL_HIST + 1 - w: POOL_HIST + 1 - w + T, sl]
        cnt = jnp.minimum(w, pos + 1).astype(jnp.float32)[None, :, None]
        means.append(s / cnt)
    pooled = (jnp.concatenate(means, axis=-1) - ext[:, POOL_HIST:]).astype(u.dtype)
    mixed = jnp.einsum('btgc,gcd->btgd', pooled.reshape(B, T, len(POOL_WINDOWS), POOL_GROUP), pool_w)
    return mixed.reshape(B, T, C) * pool_scale, ext_raw[:, -POOL_HIST:]


def trunk_layer(h, start_pos, cache_k_l, cache_v_l, conv_hist, pool_hist,
                norm_mix_g, w_in, conv_dw_w, conv_dw_b, conv_norm_g, conv_norm_b,
                conv_pw_w, conv_pw_b, pool_w, pool_scale, w_out, norm_mlp_g, w_up, w_down):
    B, T, _ = h.shape
    xn = rmsnorm(h, norm_mix_g)
    proj = xn @ w_in
    q, k, v, u_conv, u_pool = jnp.split(
        proj, [SB_WIDTH, 2 * SB_WIDTH, 3 * SB_WIDTH, 3 * SB_WIDTH + 2 * CONV_WIDTH], axis=-1)
    to_heads = lambda t: t.reshape(B, T, SB_HEADS, SB_HEAD_DIM).transpose(0, 2, 1, 3)
    q, k, v = to_heads(q), to_heads(k), to_heads(v)
    if cache_k_l is None:
        o_sb = sb_prompt(q, k, v)
    else:
        o_sb = sb_sample(q, k, v, cache_k_l, cache_v_l)
    o_sb = o_sb.transpose(0, 2, 1, 3).reshape(B, T, SB_WIDTH)
    o_conv, conv_state = conv_module(u_conv, conv_hist, conv_dw_w, conv_dw_b, conv_norm_g,
                                     conv_norm_b, conv_pw_w, conv_pw_b)
    o_pool, pool_state = pool_mixer(u_pool, pool_hist, start_pos, pool_w, pool_scale)
    h = h + jnp.concatenate([o_sb, o_conv, o_pool], axis=-1) @ w_out
    hn = rmsnorm(h, norm_mlp_g)
    h = h + jnp.square(jax.nn.relu(hn @ w_up)) @ w_down
    return h, k, v, conv_state, pool_state


def setup_inputs(seed: int = 0) -> dict:
    key = jax.random.key(seed)
    ks = jax.random.split(key, 24)
    f = jnp.float32
    nrm = lambda k, shape, s: jax.random.normal(k, shape, f) * s
    return {
        "x_prompt": nrm(ks[0], (BATCH, SEQ, D_MODEL), 1.0),
        "x_sample": nrm(ks[1], (DEC_BATCH, DEC_SEQ, D_MODEL), 1.0),
        "cache_k": nrm(ks[2], (DEPTH, DEC_BATCH, SB_HEADS, PAST_LEN, SB_HEAD_DIM), 1.0),
        "cache_v": nrm(ks[3], (DEPTH, DEC_BATCH, SB_HEADS, PAST_LEN, SB_HEAD_DIM), 1.0),
        "cache_conv": nrm(ks[4], (DEPTH, DEC_BATCH, CONV_KERNEL - 1, CONV_WIDTH), 0.5),
        "state_pool": nrm(ks[5], (DEPTH, DEC_BATCH, POOL_HIST, POOL_WIDTH), 1.0),
        "norm_mix_g": 1.0 + nrm(ks[6], (DEPTH, D_MODEL), 0.02),
        "w_in": nrm(ks[7], (DEPTH, D_MODEL, IN_COLS), D_MODEL ** -0.5),
        "conv_dw_w": nrm(ks[8], (DEPTH, CONV_KERNEL, CONV_WIDTH), CONV_KERNEL ** -0.5),
        "conv_dw_b": nrm(ks[9], (DEPTH, CONV_WIDTH), 0.02),
        "conv_norm_g": 1.0 + nrm(ks[10], (DEPTH, CONV_WIDTH), 0.02),
        "conv_norm_b": nrm(ks[11], (DEPTH, CONV_WIDTH), 0.02),
        "conv_pw_w": nrm(ks[12], (DEPTH, CONV_WIDTH, CONV_WIDTH), CONV_WIDTH ** -0.5),
        "conv_pw_b": nrm(ks[13], (DEPTH, CONV_WIDTH), 0.02),
        "pool_w": nrm(ks[14], (DEPTH, len(POOL_WINDOWS), POOL_GROUP, POOL_GROUP), POOL_GROUP ** -0.5),
        "pool_scale": 1.0 + nrm(ks[15], (DEPTH, POOL_WIDTH), 0.1),
        "w_out": nrm(ks[16], (DEPTH, D_MIX, D_MODEL), D_MIX ** -0.5),
        "norm_mlp_g": 1.0 + nrm(ks[17], (DEPTH, D_MODEL), 0.02),
        "w_up": nrm(ks[18], (DEPTH, D_MODEL, D_FF), D_MODEL ** -0.5),
        "w_down": nrm(ks[19], (DEPTH, D_FF, D_MODEL), D_FF ** -0.5),
        "final_norm_g": 1.0 + nrm(ks[20], (D_MODEL,), 0.02),
    }


def reference(x_prompt, x_sample, cache_k, cache_v, cache_conv, state_pool,
              norm_mix_g, w_in, conv_dw_w, conv_dw_b, conv_norm_g, conv_norm_b,
              conv_pw_w, conv_pw_b, pool_w, pool_scale, w_out, norm_mlp_g, w_up, w_down,
              final_norm_g):
    Bp = x_prompt.shape[0]
    past = cache_k.shape[3]
    h = x_prompt
    kp, vp, cp, pp = [], [], [], []
    for l in range(DEPTH):
        lp = (norm_mix_g[l], w_in[l], conv_dw_w[l], conv_dw_b[l], conv_norm_g[l], conv_norm_b[l],
              conv_pw_w[l], conv_pw_b[l], pool_w[l], pool_scale[l], w_out[l], norm_mlp_g[l],
              w_up[l], w_down[l])
        conv0 = jnp.zeros((Bp, CONV_KERNEL - 1, CONV_WIDTH), x_prompt.dtype)
        pool0 = jnp.zeros((Bp, POOL_HIST, POOL_WIDTH), x_prompt.dtype)
        h, k_l, v_l, c_l, p_l = trunk_layer(h, 0, None, None, conv0, pool0, *lp)
        kp.append(k_l); vp.append(v_l); cp.append(c_l); pp.append(p_l)
    y_prompt = rmsnorm(h, final_norm_g)
    h = x_sample
    ks_, vs_, cs_, ps_ = [], [], [], []
    for l in range(DEPTH):
        lp = (norm_mix_g[l], w_in[l], conv_dw_w[l], conv_dw_b[l], conv_norm_g[l], conv_norm_b[l],
              conv_pw_w[l], conv_pw_b[l], pool_w[l], pool_scale[l], w_out[l], norm_mlp_g[l],
              w_up[l], w_down[l])
        h, k_l, v_l, c_l, p_l = trunk_layer(h, past, cache_k[l], cache_v[l], cache_conv[l],
                                            state_pool[l], *lp)
        ks_.append(k_l); vs_.append(v_l); cs_.append(c_l); ps_.append(p_l)
    y_sample = rmsnorm(h, final_norm_g)
    return (y_prompt, y_sample,
            jnp.stack(kp), jnp.stack(vp), jnp.stack(cp), jnp.stack(pp),
            jnp.stack(ks_), jnp.stack(vs_), jnp.stack(cs_), jnp.stack(ps_))
```

```python
from contextlib import ExitStack
import numpy as np
import ml_dtypes
import concourse.bass as bass
import concourse.mybir as mybir
from concourse.bass_utils import run_bass_kernel_spmd

F32 = mybir.dt.float32
BF16 = mybir.dt.bfloat16
AF = mybir.ActivationFunctionType
ALU = mybir.AluOpType

NC = 8
D = 2048
NT = 1152
NP = 1024
KC = 16
TT = [(0, 384), (384, 384), (768, 384)]
EPS = 1e-6
SCALE = 128 ** -0.5
KILL = -30000.0
DBG = {}


class Trk:
    def __init__(self, nc):
        self.nc = nc
        self.eng = {"pe": nc.tensor, "act": nc.scalar, "dve": nc.vector, "pool": nc.gpsimd, "sp": nc.sync}
        self.sem = {e: nc.alloc_semaphore("c_" + e) for e in ("pe", "act", "dve", "pool")}
        self.cnt = {e: 0 for e in self.sem}
        self.seen = {e: {} for e in self.eng}
        self.lastw = {}
        self.readers = {}
        self.NDQ = {"pool": 10, "sp": 20}
        self.dsem = {q: [nc.alloc_semaphore("d%s%d" % (q, i)) for i in range(n)] for q, n in self.NDQ.items()}
        self.dval = {q: [0] * n for q, n in self.NDQ.items()}
        self.di = {q: 0 for q in self.NDQ}
        self.ccsem = nc.alloc_semaphore("ccsem")
        self.cccnt = 0

    def _semof(self, src):
        if src == "cc":
            return self.ccsem
        return self.sem[src] if isinstance(src, str) else self.dsem[src[1]][src[2]]

    def allgather(self, src_ap, dst_ap, reads, writes):
        self.deps("pool", reads, writes)
        self.nc.gpsimd.collective_compute("AllGather", ALU.bypass, replica_groups=[list(range(NC))],
                                          ins=[src_ap.opt()], outs=[dst_ap.opt()]).then_inc(self.ccsem)
        self.cccnt += 1
        self.record(("cc", self.cccnt), reads, writes)

    def _wait(self, e, src, val):
        if self.seen[e].get(src, 0) >= val:
            return
        if src == e and e == "pe":
            return
        self.eng[e].wait_ge(self._semof(src), val)
        self.seen[e][src] = val

    def deps(self, e, reads, writes):
        need = {}
        for k in reads:
            w = self.lastw.get(k)
            if w:
                need[w[0]] = max(need.get(w[0], 0), w[1])
        for k in writes:
            w = self.lastw.get(k)
            if w:
                need[w[0]] = max(need.get(w[0], 0), w[1])
            for r in self.readers.get(k, ()):
                need[r[0]] = max(need.get(r[0], 0), r[1])
        for src, val in need.items():
            self._wait(e, src, val)

    def record(self, tag, reads, writes):
        for k in reads:
            self.readers.setdefault(k, []).append(tag)
        for k in writes:
            self.lastw[k] = tag
            self.readers[k] = []

    def op(self, e, reads, writes, fn):
        self.deps(e, reads, writes)
        ins = fn(self.eng[e])
        self.cnt[e] += 1
        ins.then_inc(self.sem[e], 1)
        self.record((e, self.cnt[e]), reads, writes)

    def dma(self, q, out, in_, reads, writes):
        k = self.di[q]
        self.di[q] = (k + 1) % self.NDQ[q]
        src = ("d", q, k)
        if self.dval[q][k]:
            self._wait(q, src, self.dval[q][k])
        self.deps(q, reads, writes)
        self.dval[q][k] += 16
        self.eng[q].dma_start(out=out, in_=in_).then_inc(self.dsem[q][k], 16)
        self.record((src, self.dval[q][k]), reads, writes)

    def barrier(self):
        for e in self.eng:
            for s in self.sem:
                if s != e and self.cnt[s]:
                    self._wait(e, s, self.cnt[s])
            for q, n in self.NDQ.items():
                for k in range(n):
                    if self.dval[q][k]:
                        self._wait(e, ("d", q, k), self.dval[q][k])
        keep = {k: v for k, v in self.lastw.items() if v[0] == "cc"}
        self.lastw = keep
        self.readers = {}

    def finish(self):
        for q, n in self.NDQ.items():
            for k in range(n):
                if self.dval[q][k]:
                    self._wait("sp", ("d", q, k), self.dval[q][k])
        for s in self.sem:
            if self.cnt[s]:
                self._wait("sp", s, self.cnt[s])
        if self.cccnt:
            self._wait("sp", "cc", self.cccnt)


def build():
    nc = bass.Bass("TRN2", target_bir_lowering=False)
    T = Trk(nc)

    def dint(name, shape, dt=F32):
        return nc.dram_tensor(name, list(shape), dt, kind="Internal").ap()

    def din(name, shape, dt=F32):
        return nc.dram_tensor(name, list(shape), dt, kind="ExternalInput").ap()

    def dout(name, shape, dt=F32):
        return nc.dram_tensor(name, list(shape), dt, kind="ExternalOutput").ap()

    scopes = []
    cur = [16512]
    DTB = {F32: 4, BF16: 2}

    def sb(name, shape, dt=F32):
        nbytes = DTB[dt]
        for d in shape[1:]:
            nbytes *= d
        off = cur[0]
        cur[0] = (off + nbytes + 63) // 64 * 64
        assert cur[0] <= 229376, ("SBUF overflow", name, cur[0])
        return nc.alloc_sbuf_tensor_at("s_" + name, list(shape), dt, offset=off).ap()

    def push():
        scopes.append(cur[0])

    def pop():
        T.barrier()
        cur[0] = scopes.pop()

    hT = sb("hT", [128, KC, NT])
    xm = sb("xm", [128, KC, NT], BF16)
    ones32 = sb("ones32", [128, 128])
    ps = [nc.alloc_psum_tensor("ps%d" % i, [128, 512], F32).ap() for i in range(8)]
    PSK = ["ps%d" % i for i in range(8)]

    ones_d = din("ones32", [128, 128])
    T.dma("sp", ones32, ones_d, [], ["ones32"])
    ones16 = sb("ones16", [128, 128], BF16)
    ident16 = sb("ident16", [128, 128], BF16)
    T.dma("pool", ones16, ones_d, [], ["ones16"])
    T.dma("sp", ident16, din("ident16", [128, 128], BF16), [], ["ident16"])

    xT = din("xT", [128, KC, NT])
    T.dma("sp", hT, xT, [], ["hT"])
    SC = {}
    CONST_D = (din("negT", [128, 128], BF16), din("negU", [128, 128], BF16), din("masks", [128, 4, 512]),
               din("mask_s", [64, 512]), din("killb", [128, 7]), din("invcnt", [128, 4, 16]), din("sel", [128, 8]))

    def rmsnorm(g_d, out_ap, out_key, tagp):
        g = sb(tagp + "_g", [128, KC])
        sq = [sb(tagp + "_sq%d" % i, [128, NT], BF16) for i in range(2)]
        rstd = sb(tagp + "_rstd", [128, NT])
        T.dma("sp", g, g_d, [], [tagp + "_g"])
        for kc in range(KC):
            s = kc % 2
            T.op("act", ["hT"], [tagp + "_sq%d" % s],
                 lambda e, kc=kc, s=s: e.activation(out=sq[s], in_=hT[:, kc, :], func=AF.Square))
            for ti, (t0, tn) in enumerate(TT):
                T.op("pe", [tagp + "_sq%d" % s, "ones16"], [PSK[ti]],
                     lambda e, kc=kc, s=s, ti=ti, t0=t0, tn=tn: e.matmul(
                         ps[ti][:, 0:tn], ones16, sq[s][:, t0:t0 + tn], start=(kc == 0), stop=(kc == KC - 1)))
        for ti, (t0, tn) in enumerate(TT):
            T.op("dve", [PSK[ti]], [tagp + "_rstd"],
                 lambda e, ti=ti, t0=t0, tn=tn: e.tensor_scalar(
                     out=rstd[:, t0:t0 + tn], in0=ps[ti][:, 0:tn], scalar1=1.0 / D, scalar2=EPS,
                     op0=ALU.mult, op1=ALU.add))
        T.op("act", [tagp + "_rstd"], [tagp + "_rstd"], lambda e: e.activation(out=rstd, in_=rstd, func=AF.Sqrt))
        T.op("dve", [tagp + "_rstd"], [tagp + "_rstd"], lambda e: e.reciprocal(out=rstd, in_=rstd))
        for kc in range(KC):
            T.op("dve", ["hT", tagp + "_rstd", tagp + "_g"], [out_key],
                 lambda e, kc=kc: e.scalar_tensor_tensor(
                     out=out_ap[:, kc, :], in0=hT[:, kc, :], scalar=g[:, kc:kc + 1], in1=rstd,
                     op0=ALU.mult, op1=ALU.mult))

    wbuf = [None, None]
    wctr = [0]

    def alloc_wbuf(tag):
        for i in range(2):
            wbuf[i] = sb("wbuf%s%d" % (tag, i), [128, KC, 512], BF16)

    def load_w(w_d, r0, c0):
        s = wctr[0] % 2
        wctr[0] += 1
        src = w_d[r0:r0 + 2048, c0:c0 + 512].rearrange("(kc p) n -> p kc n", p=128)
        T.dma("pool", wbuf[s], src, [], ["wbuf%d" % s])
        return s

    evac_ctr = [0]

    def evac_engine():
        evac_ctr[0] += 1
        return "act" if evac_ctr[0] % 2 else "dve"

    def copy_op(e_name, out_ap, in_ap, reads, writes):
        if e_name == "act":
            T.op("act", reads, writes, lambda e: e.activation(out=out_ap, in_=in_ap, func=AF.Copy))
        else:
            T.op("dve", reads, writes, lambda e: e.tensor_copy(out=out_ap, in_=in_ap))

    psr = [0]

    def next_ps(lo=0, hi=8):
        i = lo + psr[0] % (hi - lo)
        psr[0] += 1
        return i

    def fm_mm(slot, mi, t0, tn, rhs_t, rhs_key, bank):
        def f(e):
            ins = None
            for kc in range(KC):
                ins = e.matmul(ps[bank][:, 0:tn], wbuf[slot][:, kc, mi * 128:(mi + 1) * 128],
                               rhs_t[:, kc, t0:t0 + tn], start=(kc == 0), stop=(kc == KC - 1))
            return ins
        T.op("pe", ["wbuf%d" % slot, rhs_key], [PSK[bank]], f)

    def proj_phase(l):
        lname = "L%d" % l
        g_d = din(lname + "_gmix", [128, KC])
        w_in = din(lname + "_w_in", [D, 4608])
        qT_o = dint("qT_d%d" % l, [8, 128, NT], BF16)
        kT32_o = dout("kT32_o%d" % l, [8, 128, NT])
        v32_o = dout("v32_o%d" % l, [NT, 1024])
        glu_o = dout("glu_o%d" % l, [4, 128, NT])
        up_o = dout("up_o%d" % l, [4, 128, NT])
        srcK = dint("srcK%d" % l, [128, 8 * NP], BF16)
        srcV = dint("srcV%d" % l, [128, 8 * NP], BF16)
        srcH = dint("srcH%d" % l, [128, 256])
        dstK = dint("dstK%d" % l, [NC * 128, 8 * NP], BF16)
        dstV = dint("dstV%d" % l, [NC * 128, 8 * NP], BF16)
        dstH = dint("dstH%d" % l, [NC * 128, 256])
        kTs_d = dint("kTs_d%d" % l, [8, 128, 128], BF16)
        vs_d = dint("vs_d%d" % l, [128, 1024], BF16)
        SC[l] = dict(qT=qT_o, glu=glu_o, up=up_o, srcK=srcK, srcV=srcV, srcH=srcH, dstK=dstK, dstV=dstV, dstH=dstH,
                     kTs=kTs_d, vs=vs_d)
        push()
        alloc_wbuf("p")
        rmsnorm(g_d, xm, "xm", lname + "n")
        if DBG.get("cut") == "norm":
            pop()
            return
        akeep = sb(lname + "_akeep", [128, 4, NT])
        zpad = sb(lname + "_zpad", [128, 76])
        T.op("dve", [], [lname + "_zpad"], lambda e: e.memset(zpad, 0.0))
        T.dma("sp", srcH[:, 180:256], zpad, [lname + "_zpad"], [])
        st16 = [sb(lname + "_st16_%d" % i, [128, 512], BF16) for i in range(2)]
        st32 = [sb(lname + "_st32_%d" % i, [128, 512]) for i in range(3)]
        c16 = [0]
        c32 = [0]
        slots = {}
        nxt = load_w(w_in, 0, 0)
        for b in range(DBG.get("nblk", 9)):
            slot = nxt
            if b + 1 < 9:
                nxt = load_w(w_in, 0, (b + 1) * 512)
            if b in (4, 5):
                for i in range(9):
                    bank = next_ps()
                    def f(e, i=i, bank=bank, slot=slot):
                        ins = None
                        for kc in range(KC):
                            ins = e.matmul(ps[bank], xm[:, kc, i * 128:(i + 1) * 128], wbuf[slot][:, kc, :],
                                           start=(kc == 0), stop=(kc == KC - 1))
                        return ins
                    T.op("pe", ["wbuf%d" % slot, "xm"], [PSK[bank]], f)
                    s32 = c32[0] % 3; c32[0] += 1
                    s16 = c16[0] % 2; c16[0] += 1
                    k32, k16 = lname + "_st32_%d" % s32, lname + "_st16_%d" % s16
                    copy_op("act", st32[s32], ps[bank], [PSK[bank]], [k32])
                    copy_op("dve", st16[s16], st32[s32], [k32], [k16])
                    cs = (b - 4) * 512
                    T.dma("sp", v32_o[i * 128:(i + 1) * 128, cs:cs + 512], st32[s32], [k32], [])
                    if i < 8:
                        dv = srcV[:, (b - 4) * 4096:(b - 3) * 4096].rearrange("p (hh x) -> p hh x", x=1024)[:, :, i * 128:(i + 1) * 128]
                        T.dma("sp", dv, st16[s16].rearrange("p (hh d) -> p hh d", d=128), [k16], [])
                    else:
                        T.dma("sp", vs_d[:, cs:cs + 512], st16[s16], [k16], [])
                continue
            for mi in range(4):
                for (t0, tn) in TT:
                    bank = next_ps()
                    fm_mm(slot, mi, t0, tn, xm, "xm", bank)
                    pin = ps[bank][:, 0:tn]
                    if b in (0, 1):
                        hh = b * 4 + mi
                        s16 = c16[0] % 2; c16[0] += 1
                        k16 = lname + "_st16_%d" % s16
                        copy_op(evac_engine(), st16[s16][:, 0:tn], pin, [PSK[bank]], [k16])
                        T.dma("sp", qT_o[hh, :, t0:t0 + tn], st16[s16][:, 0:tn], [k16], [])
                    elif b in (2, 3):
                        hh = (b - 2) * 4 + mi
                        s32 = c32[0] % 3; c32[0] += 1
                        s16 = c16[0] % 2; c16[0] += 1
                        k32, k16 = lname + "_st32_%d" % s32, lname + "_st16_%d" % s16
                        copy_op("act", st32[s32][:, 0:tn], pin, [PSK[bank]], [k32])
                        copy_op("dve", st16[s16][:, 0:tn], st32[s32][:, 0:tn], [k32], [k16])
                        T.dma("sp", kT32_o[hh, :, t0:t0 + tn], st32[s32][:, 0:tn], [k32], [])
                        if t0 < 768:
                            T.dma("sp", srcK[:, hh * NP + t0:hh * NP + t0 + tn], st16[s16][:, 0:tn], [k16], [])
                        else:
                            T.dma("sp", srcK[:, hh * NP + 768:hh * NP + 1024], st16[s16][:, 0:256], [k16], [])
                            T.dma("sp", kTs_d[hh], st16[s16][:, 256:384], [k16], [])
                    elif b == 6:
                        copy_op(evac_engine(), akeep[:, mi, t0:t0 + tn], pin, [PSK[bank]], [lname + "_akeep"])
                    elif b == 7:
                        s32 = c32[0] % 3; c32[0] += 1
                        k32 = lname + "_st32_%d" % s32
                        T.op("act", [PSK[bank]], [k32],
                             lambda e, s32=s32, pin=pin, tn=tn: e.activation(out=st32[s32][:, 0:tn], in_=pin, func=AF.Sigmoid))
                        T.op("dve", [k32, lname + "_akeep"], [k32],
                             lambda e, s32=s32, mi=mi, t0=t0, tn=tn: e.tensor_tensor(
                                 out=st32[s32][:, 0:tn], in0=st32[s32][:, 0:tn], in1=akeep[:, mi, t0:t0 + tn], op=ALU.mult))
                        T.dma("sp", glu_o[mi, :, t0:t0 + tn], st32[s32][:, 0:tn], [k32], [])
                        if t0 == 768:
                            T.dma("sp", srcH[:, mi * 45:mi * 45 + 30], st32[s32][:, 226:256], [k32], [])
                    else:
                        s32 = c32[0] % 3; c32[0] += 1
                        k32 = lname + "_st32_%d" % s32
                        copy_op(evac_engine(), st32[s32][:, 0:tn], pin, [PSK[bank]], [k32])
                        T.dma("sp", up_o[mi, :, t0:t0 + tn], st32[s32][:, 0:tn], [k32], [])
                        if t0 == 768:
                            T.dma("sp", srcH[:, mi * 45 + 30:mi * 45 + 45], st32[s32][:, 241:256], [k32], [])
        pop()
        T.allgather(srcH, dstH, [], ["dstH%d" % l])
        T.allgather(srcK, dstK, [], ["dstK%d" % l])
        T.allgather(srcV, dstV, [], ["dstV%d" % l])

    def post_phase(l):
        lname = "L%d" % l
        sc = SC[l]
        qT_i, glu_i, up_i = sc["qT"], sc["glu"], sc["up"]
        srcK, srcV, dstK, dstV, dstH, kTs_d, vs_d = sc["srcK"], sc["srcV"], sc["dstK"], sc["dstV"], sc["dstH"], sc["kTs"], sc["vs"]
        kT_c = din(lname + "_kT_c", [2, 8, 128, 2048])
        v_c = din(lname + "_v_c", [2, 8, 2048, 128])
        cconv = din(lname + "_cconv", [2, 4, 128, 30])
        cpool = din(lname + "_cpool", [2, 4, 128, 15])
        dww_d = din(lname + "_dww", [128, 4, 31])
        dwb_d = din(lname + "_dwb", [128, 4])
        ng_d = din(lname + "_ng", [128, 4])
        nb_d = din(lname + "_nb", [128, 4])
        pww_d = din(lname + "_pww", [512, 512])
        pwb_d = din(lname + "_pwb", [128, 4])
        poolw_d = din(lname + "_poolw", [4, 128, 128])
        pscale_d = din(lname + "_pscale", [128, 4])
        w_out = din(lname + "_w_out", [D, D])
        gmlp_d = din(lname + "_gmlp", [128, KC])
        w_up = din(lname + "_w_up", [D, 8192])
        w_down = din(lname + "_w_down", [8192, D])
        negT_d, negU_d, masks_d, masks_s_d, killb_d, invcnt_d, sel_d = CONST_D

        push()
        dww = sb("dww", [128, 4, 31]); dwb = sb("dwb", [128, 4]); ng = sb("ng", [128, 4]); nb = sb("nb", [128, 4])
        pwb = sb("pwb", [128, 4]); pscale = sb("pscale", [128, 4]); invcnt = sb("invcnt", [128, 4, 16])
        pww = sb("pww", [128, 4, 512], BF16)
        poolw = sb("poolw", [128, 4, 128], BF16)
        for (t_, d_, k_) in ((dww, dww_d, "dww"), (dwb, dwb_d, "dwb"), (ng, ng_d, "ng"), (nb, nb_d, "nb"),
                             (pwb, pwb_d, "pwb"), (pscale, pscale_d, "pscale"), (invcnt, invcnt_d, "invcnt")):
            T.dma("sp", t_, d_, [], [k_])
        T.dma("pool", pww, pww_d.rearrange("(cc p) n -> p cc n", p=128), [], ["pww"])
        T.dma("pool", poolw, poolw_d.rearrange("g p d -> p g d"), [], ["poolw"])
        halall = sb("halall", [128, 8, 180])
        selt = sb("selt", [128, 8])
        hal = sb("hal", [128, 180])
        T.dma("sp", selt, sel_d, [], ["selt"])
        T.dma("sp", halall, dstH.rearrange("(r p) c -> p r c", p=128)[:, :, 0:180], ["dstH%d" % l], ["halall"])
        T.op("dve", ["halall", "selt"], ["hal"],
             lambda e: e.tensor_scalar(out=hal, in0=halall[:, 0, :], scalar1=selt[:, 0:1], scalar2=None, op0=ALU.mult))
        for r in range(1, 8):
            T.op("dve", ["halall", "selt", "hal"], ["hal"],
                 lambda e, r=r: e.scalar_tensor_tensor(out=hal, in0=halall[:, r, :], scalar=selt[:, r:r + 1], in1=hal,
                                                       op0=ALU.mult, op1=ALU.add))
        NE = 30 + NP + 2 * (30 + 64)
        NY = NE - 30
        YT = [(0, 404), (404, 404), (808, 404)]
        ext = [sb("ext%d" % i, [128, NE]) for i in range(2)]
        acc = [sb("acc%d" % i, [128, NY]) for i in range(2)]
        extb = [sb("extb%d" % i, [128, NE], BF16) for i in range(2)]
        dwd = [sb("dwd%d" % i, [128, 31, 128], BF16) for i in range(2)]
        dd = sb("cdd", [128, NY]); dsq = sb("cdsq", [128, NY]); crs = sb("crs", [128, NY])
        ynT = sb("ynT", [128, 4, NY], BF16)
        for cc in range(4):
            b2 = cc % 2
            ek, ak = "ext%d" % b2, "acc%d" % b2
            X_, A_ = ext[b2], acc[b2]
            T.op("dve", ["hal"], [ek], lambda e, X_=X_, cc=cc: e.tensor_copy(out=X_[:, 0:30], in_=hal[:, cc * 45:cc * 45 + 30]))
            T.dma("sp", X_[:, 30:30 + NP], glu_i[cc, :, 0:NP], [], [ek])
            for st in range(2):
                o = 30 + NP + st * 94
                T.dma("sp", X_[:, o:o + 30], cconv[st, cc], [], [ek])
                T.dma("sp", X_[:, o + 30:o + 94], glu_i[cc, :, NP + 64 * st:NP + 64 * st + 64], [], [ek])
            xb, xbk = extb[b2], "extb%d" % b2
            dd_, ddk = dwd[b2], "dwd%d" % b2
            T.op("act", [ek], [xbk], lambda e, X_=X_, xb=xb: e.activation(out=xb, in_=X_, func=AF.Copy))
            for j in range(31):
                T.op("dve", ["ident16", "dww"], [ddk],
                     lambda e, dd_=dd_, cc=cc, j=j: e.tensor_scalar(out=dd_[:, j, :], in0=ident16, scalar1=dww[:, cc, j:j + 1],
                                                                    scalar2=None, op0=ALU.mult))
            for ti, (y0, yn) in enumerate(YT):
                bank = next_ps()
                def tf(e, xb=xb, dd_=dd_, y0=y0, yn=yn, bank=bank):
                    ins = None
                    for j in range(31):
                        ins = e.matmul(ps[bank][:, 0:yn], dd_[:, j, :], xb[:, y0 + j:y0 + j + yn], start=(j == 0), stop=(j == 30))
                    return ins
                T.op("pe", [xbk, ddk], [PSK[bank]], tf)
                T.op("dve", [PSK[bank], "dwb"], [ak],
                     lambda e, A_=A_, cc=cc, y0=y0, yn=yn, bank=bank: e.tensor_scalar(
                         out=A_[:, y0:y0 + yn], in0=ps[bank][:, 0:yn], scalar1=dwb[:, cc:cc + 1], scalar2=None, op0=ALU.add))
            for ti, (y0, yn) in enumerate(YT):
                T.op("pe", [ak, "ones32"], [PSK[ti]],
                     lambda e, A_=A_, ti=ti, y0=y0, yn=yn: e.matmul(ps[ti][:, 0:yn], ones32, A_[:, y0:y0 + yn], start=True, stop=True))
                T.op("dve", [PSK[ti], ak], ["cdd"],
                     lambda e, A_=A_, ti=ti, y0=y0, yn=yn: e.scalar_tensor_tensor(
                         out=dd[:, y0:y0 + yn], in0=ps[ti][:, 0:yn], scalar=-1.0 / 128, in1=A_[:, y0:y0 + yn],
                         op0=ALU.mult, op1=ALU.add))
            T.op("act", ["cdd"], ["cdsq"], lambda e: e.activation(out=dsq, in_=dd, func=AF.Square))
            for ti, (y0, yn) in enumerate(YT):
                T.op("pe", ["cdsq", "ones32"], [PSK[ti]],
                     lambda e, ti=ti, y0=y0, yn=yn: e.matmul(ps[ti][:, 0:yn], ones32, dsq[:, y0:y0 + yn], start=True, stop=True))
                T.op("dve", [PSK[ti]], ["crs"],
                     lambda e, ti=ti, y0=y0, yn=yn: e.tensor_scalar(out=crs[:, y0:y0 + yn], in0=ps[ti][:, 0:yn],
                                                                    scalar1=1.0 / 128, scalar2=EPS, op0=ALU.mult, op1=ALU.add))
            T.op("act", ["crs"], ["crs"], lambda e: e.activation(out=crs, in_=crs, func=AF.Sqrt))
            T.op("dve", ["crs"], ["crs"], lambda e: e.reciprocal(out=crs, in_=crs))
            T.op("dve", ["cdd", "crs"], ["cdd"], lambda e: e.tensor_tensor(out=dd, in0=dd, in1=crs, op=ALU.mult))
            T.op("dve", ["cdd", "ng", "nb"], ["cdd"],
                 lambda e, cc=cc: e.tensor_scalar(out=dd, in0=dd, scalar1=ng[:, cc:cc + 1], scalar2=nb[:, cc:cc + 1],
                                                  op0=ALU.mult, op1=ALU.add))
            T.op("act", ["cdd"], ["ynT"], lambda e, cc=cc: e.activation(out=ynT[:, cc, :], in_=dd, func=AF.Silu))
        SEG = [(0, 512, 0), (512, 512, 512), (NP + 30, 64, NP), (NP + 124, 64, NP + 64)]
        for m in range(4):
            for (y0, n, tc) in SEG:
                bank = next_ps()
                def f(e, m=m, y0=y0, n=n, bank=bank):
                    ins = None
                    for cc in range(4):
                        ins = e.matmul(ps[bank][:, 0:n], pww[:, cc, m * 128:(m + 1) * 128], ynT[:, cc, y0:y0 + n],
                                       start=(cc == 0), stop=(cc == 3))
                    return ins
                T.op("pe", ["pww", "ynT"], [PSK[bank]], f)
                T.op("dve", [PSK[bank], "pwb"], ["xm"],
                     lambda e, m=m, n=n, tc=tc, bank=bank: e.tensor_scalar(
                         out=xm[:, 8 + m, tc:tc + n], in0=ps[bank][:, 0:n], scalar1=pwb[:, m:m + 1], scalar2=None, op0=ALU.add))

        NEP = 15 + NP + 2 * (15 + 64)
        pext = [sb("pext%d" % i, [128, NEP]) for i in range(2)]
        pa = sb("pa", [128, NEP]); pb = sb("pb", [128, NEP])
        T.op("dve", [], ["pa"], lambda e: e.memset(pa, 0.0))
        T.op("dve", [], ["pb"], lambda e: e.memset(pb, 0.0))
        pooled = sb("pooled", [128, NT], BF16)
        p16 = sb("p16", [128, 16])
        for g in range(4):
            w = 2 << g
            b2 = g % 2
            pk = "pext%d" % b2
            P_ = pext[b2]
            T.op("dve", ["hal"], [pk], lambda e, P_=P_, g=g: e.tensor_copy(out=P_[:, 0:15], in_=hal[:, g * 45 + 30:g * 45 + 45]))
            T.dma("sp", P_[:, 15:15 + NP], up_i[g, :, 0:NP], [], [pk])
            for st in range(2):
                o = 15 + NP + st * 79
                T.dma("sp", P_[:, o:o + 15], cpool[st, g], [], [pk])
                T.dma("sp", P_[:, o + 15:o + 79], up_i[g, :, NP + 64 * st:NP + 64 * st + 64], [], [pk])
            src, srck = P_, pk
            sh = 1
            flip = 0
            while sh < w:
                dst, dstk = (pa, "pa") if flip == 0 else (pb, "pb")
                T.op("dve", [srck], [dstk],
                     lambda e, src=src, dst=dst, sh=sh: e.tensor_tensor(out=dst[:, sh:NEP], in0=src[:, sh:NEP],
                                                                      in1=src[:, 0:NEP - sh], op=ALU.add))
                src, srck = dst, dstk
                sh *= 2
                flip ^= 1
            for (o, n, tc) in ((15, NP, 0), (15 + NP + 15, 64, NP), (15 + NP + 79 + 15, 64, NP + 64)):
                T.op("dve", [srck, pk], ["pooled"],
                     lambda e, src=src, P_=P_, o=o, n=n, tc=tc, w=w: e.scalar_tensor_tensor(
                         out=pooled[:, tc:tc + n], in0=src[:, o:o + n], scalar=1.0 / w, in1=P_[:, o:o + n],
                         op0=ALU.mult, op1=ALU.subtract))
            T.op("dve", [srck, "invcnt"], ["pa16"],
                 lambda e, src=src, g=g: e.tensor_tensor(out=p16, in0=src[:, 15:31], in1=invcnt[:, g, :], op=ALU.mult))
            T.op("dve", ["pa16", pk], ["pooled"],
                 lambda e, P_=P_: e.tensor_tensor(out=pooled[:, 0:16], in0=p16, in1=P_[:, 15:31], op=ALU.subtract))
            for (t0, tn) in TT:
                bank = next_ps()
                T.op("pe", ["poolw", "pooled"], [PSK[bank]],
                     lambda e, g=g, t0=t0, tn=tn, bank=bank: e.matmul(ps[bank][:, 0:tn], poolw[:, g, :], pooled[:, t0:t0 + tn],
                                                                      start=True, stop=True))
                T.op("dve", [PSK[bank], "pscale"], ["xm"],
                     lambda e, g=g, t0=t0, tn=tn, bank=bank: e.tensor_scalar(
                         out=xm[:, 12 + g, t0:t0 + tn], in0=ps[bank][:, 0:tn], scalar1=pscale[:, g:g + 1], scalar2=None, op0=ALU.mult))

        pop()

        push()
        negT = sb("negT", [128, 128], BF16)
        negU = sb("negU", [128, 128], BF16)
        masks = sb("masks", [128, 4, 512])
        mask_s = sb("mask_s", [64, 512])
        killb = sb("killb", [128, 7])
        T.dma("sp", negT, negT_d, [], ["negT"])
        T.dma("sp", negU, negU_d, [], ["negU"])
        T.dma("sp", masks, masks_d, [], ["masks"])
        T.dma("sp", mask_s, masks_s_d, [], ["mask_s"])
        T.dma("sp", killb, killb_d, [], ["killb"])

        Et = [[sb("E%d_%d" % (s, i), [128, 512]) for i in range(2)] for s in range(2)]
        Lt = [[sb("L%d_%d" % (s, i), [128, 512], BF16) for i in range(3)] for s in range(2)]
        Xt = [[sb("X%d_%d" % (s, i), [128, 512]) for i in range(2)] for s in range(2)]
        Wt = [[sb("W%d_%d" % (s, i), [128, 512], BF16) for i in range(2)] for s in range(2)]
        Ed = sb("Ediag", [128, 512])
        T.op("dve", [], ["Ediag"], lambda e: e.memset(Ed, 0.0))
        stc = [0, 0]
        zc = [0]
        prevL = [None, None]

        ZB = [0, 1, 6]
        pend = []

        def step(s, zfn, zreads, bias_ap, bias_key, mask_ap, mask_key, avfn, avreads, first, nrow=128, ediag=False,
                 pre=(), post=()):
            pend.append(dict(s=s, zfn=zfn, zreads=zreads, bias_ap=bias_ap, bias_key=bias_key, mask_ap=mask_ap,
                             mask_key=mask_key, avfn=avfn, avreads=avreads, first=first, nrow=nrow, ediag=ediag,
                             pre=list(pre), post=list(post)))

        def stA(d):
            for f in d["pre"]:
                f()
            s_ = d["s"]
            i = stc[s_]; stc[s_] += 1
            zb = ZB[zc[0] % 3]; zc[0] += 1
            d["zb"] = zb
            d["ek"] = "Ediag" if d["ediag"] else "E%d_%d" % (s_, i % 2)
            d["E"] = Ed if d["ediag"] else Et[s_][i % 2]
            d["lk"] = "L%d_%d" % (s_, i % 3); d["L"] = Lt[s_][i % 3]
            d["xk"] = "X%d_%d" % (s_, i % 2); d["X"] = Xt[s_][i % 2]
            d["wk"] = "W%d_%d" % (s_, i % 2); d["W"] = Wt[s_][i % 2]
            T.op("pe", d["zreads"], [PSK[zb]], lambda e: d["zfn"](e, ps[zb]))

        def stB(d):
            zb, E, nrow, ek = d["zb"], d["E"], d["nrow"], d["ek"]
            if d["bias_ap"] is None:
                T.op("act", [PSK[zb]], [ek],
                     lambda e: e.activation(out=E[0:nrow, :], in_=ps[zb][0:nrow, :], func=AF.Exp, scale=SCALE))
            else:
                T.op("act", [PSK[zb], d["bias_key"]], [ek],
                     lambda e: e.activation(out=E[0:nrow, :], in_=ps[zb][0:nrow, :], func=AF.Exp, scale=SCALE,
                                            bias=d["bias_ap"]))
            if d["mask_ap"] is not None:
                T.op("dve", [ek, d["mask_key"]], [ek],
                     lambda e: e.tensor_tensor(out=E[0:nrow, :], in0=E[0:nrow, :], in1=d["mask_ap"], op=ALU.mult))
            T.op("act", [ek], [d["lk"]], lambda e: e.activation(out=d["L"], in_=E, func=AF.Ln, bias=1.0))

        def stC(d):
            s_ = d["s"]; first = d["first"]; sbank = 2 + s_
            pl = prevL[s_]
            def cf(e):
                if not first:
                    e.matmul(ps[sbank], negU, pl[1], start=False, stop=False, skip_group_check=True)
                return e.matmul(ps[sbank], negT, d["L"], start=first, stop=True, skip_group_check=True)
            rd = [d["lk"], "negT", "negU"] + ([] if first else [pl[0]])
            T.op("pe", rd, [PSK[sbank]], cf)
            prevL[s_] = (d["lk"], d["L"])

        def stD(d):
            s_ = d["s"]; sbank, obank = 2 + s_, 4 + s_
            T.op("act", [PSK[sbank]], [d["xk"]], lambda e: e.activation(out=d["X"], in_=ps[sbank], func=AF.Exp))
            T.op("dve", [d["ek"], d["xk"]], [d["wk"]],
                 lambda e: e.tensor_tensor(out=d["W"], in0=d["E"], in1=d["X"], op=ALU.mult))
            T.op("pe", [d["wk"]] + d["avreads"], [PSK[obank]], lambda e: d["avfn"](e, d["W"], ps[obank], d["first"]))
            for f in d["post"]:
                f()

        def flush():
            n = len(pend)
            for it in range(n + 2):
                if it < n:
                    stA(pend[it])
                if 0 <= it - 1 < n:
                    stB(pend[it - 1])
                same = (0 <= it - 2) and (it - 1 < n) and pend[it - 1]["s"] == pend[it - 2]["s"]
                if same:
                    stD(pend[it - 2])
                    stC(pend[it - 1])
                else:
                    if 0 <= it - 1 < n:
                        stC(pend[it - 1])
                    if 0 <= it - 2 < n:
                        stD(pend[it - 2])
            del pend[:]

        push()
        kcs = [sb("kcs%d" % i, [128, 8, 512], BF16) for i in range(2)]
        vcs = [sb("vcs%d" % i, [128, 8, 4, 128], BF16) for i in range(2)]
        qsm = [sb("qsm%d" % i, [128, 8, 64], BF16) for i in range(2)]
        ksm = [sb("ksm%d" % i, [128, 8, 64], BF16) for i in range(2)]
        vsm = [sb("vsm%d" % i, [64, 8, 128], BF16) for i in range(2)]

        def sm_loads(st):
            c0 = NP + 64 * st
            def f():
                T.dma("sp", qsm[st], qT_i[:, :, c0:c0 + 64].rearrange("h p t -> p h t"), [], ["qsm%d" % st])
                T.dma("sp", ksm[st], kTs_d[:, :, 64 * st:64 * st + 64].rearrange("h p t -> p h t"), [], ["ksm%d" % st])
                T.dma("sp", vsm[st], vs_d[64 * st:64 * st + 64, :].rearrange("t (h d) -> t h d", d=128), [], ["vsm%d" % st])
            return f

        def cache_loads(gq):
            st, qtr = gq // 4, 3 - (gq % 4)
            sl = gq % 2
            def f():
                T.dma("pool", kcs[sl], kT_c[st, :, :, qtr * 512:(qtr + 1) * 512].rearrange("h p t -> p h t"), [], ["kcs%d" % sl])
                for kb4 in range(4):
                    r0 = qtr * 512 + kb4 * 128
                    T.dma("pool", vcs[sl][:, :, kb4, :], v_c[st, :, r0:r0 + 128, :].rearrange("h p d -> p h d"),
                          [], ["vcs%d" % sl])
            return f

        def s_evac(st):
            c0 = NP + 64 * st
            def f():
                for hh in range(8):
                    copy_op("dve", xm[:, hh, c0:c0 + 64], ps[4][:, hh * 64:(hh + 1) * 64], [PSK[4]], ["xm"])
            return f

        stc[0] = 0
        for st in range(2):
            def zfn(e, zb, st=st):
                ins = None
                for hh in range(8):
                    ins = e.matmul(zb[0:64, hh * 64:(hh + 1) * 64], ksm[st][:, hh, :], qsm[st][:, hh, :], start=(hh == 0), stop=(hh == 7))
                return ins
            def avfn(e, W, ob, first, st=st):
                ins = None
                for hh in range(8):
                    ins = e.matmul(ob[:, hh * 64:(hh + 1) * 64], vsm[st][:, hh, :], W[0:64, hh * 64:(hh + 1) * 64],
                                   start=(hh == 0), stop=False)
                return ins
            pre = [sm_loads(st)]
            if st == 0:
                pre += [sm_loads(1), cache_loads(0)]
            step(0, zfn, ["ksm%d" % st, "qsm%d" % st], None, None, mask_s, "mask_s", avfn, ["vsm%d" % st], True,
                 nrow=64, ediag=True, pre=pre)
            for qi in range(4):
                gq = st * 4 + qi
                sl = gq % 2
                pre = [cache_loads(gq + 1)] if gq + 1 < 8 else []
                for kb in range(3, -1, -1):
                    def zfn(e, zb, sl=sl, kb=kb, st=st):
                        ins = None
                        for hh in range(8):
                            ins = e.matmul(zb[:, hh * 64:(hh + 1) * 64], kcs[sl][:, hh, kb * 128:(kb + 1) * 128], qsm[st][:, hh, :],
                                           start=(hh == 0), stop=(hh == 7))
                        return ins
                    last = (qi == 3 and kb == 0)
                    def avfn(e, W, ob, first, sl=sl, kb=kb, last=last):
                        ins = None
                        for hh in range(8):
                            ins = e.matmul(ob[:, hh * 64:(hh + 1) * 64], vcs[sl][:, hh, kb, :], W[:, hh * 64:(hh + 1) * 64],
                                           start=False, stop=(last and hh == 7))
                        return ins
                    step(0, zfn, ["kcs%d" % sl, "qsm%d" % st], None, None, None, None, avfn, ["vcs%d" % sl], False,
                         pre=(pre if kb == 1 else ()), post=([s_evac(st)] if last else ()))
        flush()
        pop()

        push()
        NSL = 4
        kch = [sb("kch%d" % i, [128, NP], BF16) for i in range(NSL)]
        vch = [sb("vch%d" % i, [128, 8, 128], BF16) for i in range(NSL)]
        qh = [sb("qh%d" % i, [128, NP], BF16) for i in range(2)]

        def chunk_loads(gc):
            h, ci = gc // 8, gc % 8
            sl = gc % NSL
            if ci == 0:
                ksrc = srcK[:, h * NP:(h + 1) * NP]
                vsrc = srcV[:, h * NP:(h + 1) * NP]
                rk, rv = [], []
            else:
                g = 7 - ci
                ksrc = dstK[g * 128:(g + 1) * 128, h * NP:(h + 1) * NP]
                vsrc = dstV[g * 128:(g + 1) * 128, h * NP:(h + 1) * NP]
                rk, rv = ["dstK%d" % l], ["dstV%d" % l]
            def f():
                T.dma("sp", kch[sl], ksrc, rk, ["kch%d" % sl])
                T.dma("sp", vch[sl], vsrc.rearrange("p (kb d) -> p kb d", d=128), rv, ["vch%d" % sl])
            return f

        def q_load(h):
            def f():
                T.dma("sp", qh[h % 2], qT_i[h, :, 0:NP], [], ["qh%d" % (h % 2)])
            return f

        def o_evac(h):
            def f():
                for s_ in range(2):
                    copy_op(evac_engine(), xm[:, h, s_ * 512:(s_ + 1) * 512], ps[4 + s_], [PSK[4 + s_]], ["xm"])
            return f

        stc[0] = stc[1] = 0
        for h in range(8):
            qs = h % 2
            started = [False, False]
            for ci in range(8):
                gc = h * 8 + ci
                sl = gc % NSL
                pre = []
                if gc == 0:
                    pre += [q_load(0), chunk_loads(0), chunk_loads(1)]
                if ci == 0 and h + 1 < 8:
                    pre.append(q_load(h + 1))
                if gc + 2 < 64:
                    pre.append(chunk_loads(gc + 2))
                firstofchunk = True
                for kb in range(7, -1, -1):
                    for s_ in range(2):
                        mask_ap = None
                        if ci == 0:
                            r = kb - 4 * s_
                            if r > 3:
                                continue
                            if r >= 0:
                                mask_ap = masks[:, r, :]
                        def zfn(e, zb, sl=sl, kb=kb, s_=s_, qs=qs):
                            return e.matmul(zb, kch[sl][:, kb * 128:(kb + 1) * 128], qh[qs][:, s_ * 512:(s_ + 1) * 512],
                                            start=True, stop=True)
                        last = (ci == 7 and kb == 0)
                        def avfn(e, W, ob, first, sl=sl, kb=kb, last=last):
                            return e.matmul(ob, vch[sl][:, kb, :], W, start=first, stop=last)
                        first = not started[s_]
                        started[s_] = True
                        step(s_, zfn, ["kch%d" % sl, "qh%d" % qs],
                             None if ci == 0 else killb[:, 7 - ci:8 - ci], "killb",
                             mask_ap, "masks", avfn, ["vch%d" % sl], first,
                             pre=(pre if firstofchunk else ()),
                             post=([o_evac(h)] if (last and s_ == 1) else ()))
                        firstofchunk = False
        flush()
        pop()
        pop()

        push()
        alloc_wbuf("m")
        nxt = load_w(w_out, 0, 0)
        for b in range(4):
            slot = nxt
            if b + 1 < 4:
                nxt = load_w(w_out, 0, (b + 1) * 512)
            else:
                nxt = load_w(w_up, 0, 0)
            for mi in range(4):
                m = b * 4 + mi
                for (t0, tn) in TT:
                    bank = next_ps()
                    fm_mm(slot, mi, t0, tn, xm, "xm", bank)
                    T.op("dve", [PSK[bank], "hT"], ["hT"],
                         lambda e, m=m, t0=t0, tn=tn, bank=bank: e.tensor_tensor(
                             out=hT[:, m, t0:t0 + tn], in0=hT[:, m, t0:t0 + tn], in1=ps[bank][:, 0:tn], op=ALU.add))
        rmsnorm(gmlp_d, xm, "xm", lname + "m")
        aT = sb("aT", [128, KC, NT], BF16)
        rl = [sb("rl%d" % i, [128, 384]) for i in range(2)]
        rc = [0]
        for q in range(4):
            for b in range(4):
                slot = nxt
                if b + 1 < 4:
                    nxt = load_w(w_up, 0, q * 2048 + (b + 1) * 512)
                else:
                    nxt = load_w(w_down, q * 2048, 0)
                for mi in range(4):
                    m = b * 4 + mi
                    for (t0, tn) in TT:
                        bank = next_ps()
                        fm_mm(slot, mi, t0, tn, xm, "xm", bank)
                        r = rc[0] % 2; rc[0] += 1
                        T.op("act", [PSK[bank]], ["rl%d" % r],
                             lambda e, r=r, tn=tn, bank=bank: e.activation(out=rl[r][:, 0:tn], in_=ps[bank][:, 0:tn], func=AF.Relu))
                        T.op("dve", ["rl%d" % r], ["aT"],
                             lambda e, r=r, m=m, t0=t0, tn=tn: e.tensor_tensor(
                                 out=aT[:, m, t0:t0 + tn], in0=rl[r][:, 0:tn], in1=rl[r][:, 0:tn], op=ALU.mult))
            for b in range(4):
                slot = nxt
                if b + 1 < 4:
                    nxt = load_w(w_down, q * 2048, (b + 1) * 512)
                elif q + 1 < 4:
                    nxt = load_w(w_up, 0, (q + 1) * 2048)
                else:
                    nxt = None
                for mi in range(4):
                    m = b * 4 + mi
                    for (t0, tn) in TT:
                        bank = next_ps()
                        fm_mm(slot, mi, t0, tn, aT, "aT", bank)
                        T.op("dve", [PSK[bank], "hT"], ["hT"],
                             lambda e, m=m, t0=t0, tn=tn, bank=bank: e.tensor_tensor(
                                 out=hT[:, m, t0:t0 + tn], in0=hT[:, m, t0:t0 + tn], in1=ps[bank][:, 0:tn], op=ALU.add))
        pop()

    for l in range(2):
        proj_phase(l)
        post_phase(l)
    gf_d = din("gfinal", [128, KC])
    yT_o = dout("yT_o", [128, KC, NT])
    push()
    rmsnorm(gf_d, hT, "hT", "fin")
    T.dma("sp", yT_o, hT, ["hT"], [])
    pop()
    T.finish()
    return nc


_PROG = []


def _prog():
    if not _PROG:
        _PROG.append(build())
    return _PROG[0]


def _pk(v):
    v = np.asarray(v, np.float32)
    return np.ascontiguousarray(v.reshape(-1, 128).T)


def _consts():
    j = np.arange(128)[:, None]
    s = np.arange(128)[None, :]
    negT = np.where(j >= s, -1.0, 0.0).astype(ml_dtypes.bfloat16)
    negU = np.where(j < s, -1.0, 0.0).astype(ml_dtypes.bfloat16)
    t = np.arange(512)[None, :]
    masks = np.stack([((128 * r + np.arange(128)[:, None]) < t) for r in range(4)], axis=1).astype(np.float32)
    ms = (np.arange(64)[:, None] < (np.arange(512)[None, :] % 64)).astype(np.float32)
    return {"negT": negT, "negU": negU, "masks": np.ascontiguousarray(masks), "mask_s": ms,
            "ident16": np.eye(128, dtype=np.float32).astype(ml_dtypes.bfloat16)}


def kernel(x_prompt, x_sample, cache_k, cache_v, cache_conv, state_pool,
           norm_mix_g, w_in, conv_dw_w, conv_dw_b, conv_norm_g, conv_norm_b,
           conv_pw_w, conv_pw_b, pool_w, pool_scale, w_out, norm_mlp_g, w_up, w_down,
           final_norm_g):
    f32 = np.float32
    A = lambda a: np.ascontiguousarray(np.asarray(a, f32))
    x_prompt, x_sample = A(x_prompt), A(x_sample)
    shared = {"ones32": np.ones((128, 128), f32), "gfinal": _pk(final_norm_g)}
    shared.update(_consts())
    for l in range(2):
        pre = "L%d" % l
        shared.update({
            pre + "_gmix": _pk(norm_mix_g[l]), pre + "_w_in": A(w_in[l]),
            pre + "_dww": np.ascontiguousarray(A(conv_dw_w[l]).T.reshape(4, 128, 31).transpose(1, 0, 2)),
            pre + "_dwb": _pk(conv_dw_b[l]), pre + "_ng": _pk(conv_norm_g[l]), pre + "_nb": _pk(conv_norm_b[l]),
            pre + "_pww": A(conv_pw_w[l]), pre + "_pwb": _pk(conv_pw_b[l]),
            pre + "_poolw": A(pool_w[l]), pre + "_pscale": _pk(pool_scale[l]),
            pre + "_w_out": A(w_out[l]), pre + "_gmlp": _pk(norm_mlp_g[l]),
            pre + "_w_up": A(w_up[l]), pre + "_w_down": A(w_down[l]),
        })
    ck = [A(cache_k[l]) for l in range(2)]; cv = [A(cache_v[l]) for l in range(2)]
    cc_ = [A(cache_conv[l]) for l in range(2)]; cp_ = [A(state_pool[l]) for l in range(2)]
    maps = []
    for c in range(NC):
        m = dict(shared)
        tok = np.concatenate([x_prompt[0, c * NP:(c + 1) * NP], x_sample[2 * c], x_sample[2 * c + 1]], axis=0)
        m["xT"] = np.ascontiguousarray(tok.T.reshape(KC, 128, NT).transpose(1, 0, 2))
        kb = np.zeros((128, 7), f32)
        kb[:, c:] = KILL
        m["killb"] = kb
        sel = np.zeros((128, 8), f32)
        if c > 0:
            sel[:, c - 1] = 1.0
        m["sel"] = sel
        ic = np.zeros((128, 4, 16), f32)
        for g in range(4):
            w = 2 << g
            pos = c * NP + np.arange(16)
            ic[:, g, :] = 1.0 / np.minimum(w, pos + 1).astype(f32)
        m["invcnt"] = ic
        for l in range(2):
            pre = "L%d" % l
            m[pre + "_kT_c"] = np.ascontiguousarray(ck[l][2 * c:2 * c + 2].transpose(0, 1, 3, 2))
            m[pre + "_v_c"] = np.ascontiguousarray(cv[l][2 * c:2 * c + 2])
            m[pre + "_cconv"] = np.ascontiguousarray(cc_[l][2 * c:2 * c + 2].transpose(0, 2, 1).reshape(2, 4, 128, 30))
            m[pre + "_cpool"] = np.ascontiguousarray(cp_[l][2 * c:2 * c + 2].transpose(0, 2, 1).reshape(2, 4, 128, 15))
        maps.append(m)

    r = run_bass_kernel_spmd(_prog(), maps, core_ids=list(range(NC))).results

    def tokmajor(yT):
        return np.asarray(yT, f32).transpose(2, 1, 0).reshape(NT, D)
    ys = [tokmajor(r[c]["yT_o"]) for c in range(NC)]
    y_prompt = np.concatenate([y[0:NP] for y in ys], axis=0)[None]
    y_sample = np.stack([ys[b // 2][NP + 64 * (b % 2):NP + 64 * (b % 2) + 64] for b in range(16)], axis=0)

    def layer_outs(l):
        K = lambda c: np.asarray(r[c]["kT32_o%d" % l], f32)
        V = lambda c: np.asarray(r[c]["v32_o%d" % l], f32)
        G = lambda c: np.asarray(r[c]["glu_o%d" % l], f32)
        U = lambda c: np.asarray(r[c]["up_o%d" % l], f32)
        kp = np.concatenate([K(c)[:, :, 0:NP] for c in range(NC)], axis=2)
        kp = np.ascontiguousarray(kp.transpose(0, 2, 1))[None]
        vp = np.concatenate([V(c)[0:NP] for c in range(NC)], axis=0)
        vp = np.ascontiguousarray(vp.reshape(8192, 8, 128).transpose(1, 0, 2))[None]
        cp = np.ascontiguousarray(G(NC - 1)[:, :, NP - 30:NP].reshape(512, 30).T)[None]
        pp = np.ascontiguousarray(U(NC - 1)[:, :, NP - 15:NP].reshape(512, 15).T)[None]
        ks, vs, cs, pss = [], [], [], []
        for b in range(16):
            c, st = b // 2, b % 2
            c0 = NP + 64 * st
            ks.append(K(c)[:, :, c0:c0 + 64].transpose(0, 2, 1))
            vs.append(V(c)[c0:c0 + 64].reshape(64, 8, 128).transpose(1, 0, 2))
            cs.append(G(c)[:, :, c0 + 34:c0 + 64].reshape(512, 30).T)
            pss.append(U(c)[:, :, c0 + 49:c0 + 64].reshape(512, 15).T)
        return kp, vp, cp, pp, np.stack(ks), np.stack(vs), np.stack(cs), np.stack(pss)

    o0 = layer_outs(0)
    o1 = layer_outs(1)
    st = lambda i: np.ascontiguousarray(np.stack([o0[i], o1[i]], axis=0).astype(f32))
    return (np.ascontiguousarray(y_prompt.astype(f32)), np.ascontiguousarray(y_sample.astype(f32)),
            st(0), st(1), st(2), st(3), st(4), st(5), st(6), st(7))
```
